# Optimizing a Trainium2 kernel written in Bass

```python
import jax, jax.numpy as jnp
from jax import lax
import numpy as np


D_MODEL = 1024
BATCH = 8
SEQ = 4096
DEPTH = 2

EPS = 1e-6
HG_HEADS = 4
HG_HEAD_DIM = 128
HG_DIM = HG_HEADS * HG_HEAD_DIM
GLA_HEADS = 4
GLA_DK = 64
GLA_DV = 128
GLA_KDIM = GLA_HEADS * GLA_DK
GLA_VDIM = GLA_HEADS * GLA_DV
GLA_RANK = 16
GLA_TAU = 16.0
LA_CHUNK = 64
NSA_HEADS = 8
NSA_KV_GROUPS = 2
NSA_REP = NSA_HEADS // NSA_KV_GROUPS
NSA_DH = 64
NSA_DIM = NSA_HEADS * NSA_DH
NSA_KV_DIM = NSA_KV_GROUPS * NSA_DH
CMP_STRIDE = 16
CMP_BLK = 2 * CMP_STRIDE
CMP_HID = 128
SLC_BLK = 64
N_SEL = 16
WIN = 512
Q_BLOCK = 128
ROPE_THETA = 500000.0
ROT_DIM = NSA_DH // 4
N_BRANCH = 3
FF_DIM = ((8 * D_MODEL // 3 + 255) // 256) * 256

IN_SPLITS = (HG_DIM, HG_DIM, HG_DIM, HG_DIM,
             GLA_KDIM, GLA_KDIM, GLA_VDIM, GLA_RANK, GLA_VDIM,
             NSA_DIM, NSA_KV_DIM, NSA_KV_DIM, NSA_KV_DIM, NSA_KV_DIM, NSA_KV_DIM, NSA_KV_DIM,
             3 * NSA_HEADS,
             N_BRANCH * D_MODEL)
IN_COLS = sum(IN_SPLITS)

kernel_name = 'hybrid_hgrn2_gla_nsa_block'


def rmsnorm(x, g):
    x32 = x.astype(jnp.float32)
    y = x32 * lax.rsqrt(jnp.mean(x32 * x32, axis=-1, keepdims=True) + EPS)
    return y.astype(x.dtype) * g


def split_cols(z):
    outs = []
    start = 0
    for n in IN_SPLITS:
        outs.append(z[..., start:start + n])
        start += n
    return outs


def heads(z, n):
    B, T, _ = z.shape
    return z.reshape(B, T, n, -1).transpose(0, 2, 1, 3)


def head_rmsnorm(o, g, dtype):
    B, H, T, dv = o.shape
    o = o.astype(jnp.float32).transpose(0, 2, 1, 3)
    o = o * lax.rsqrt(jnp.mean(o * o, axis=-1, keepdims=True) + EPS)
    return o.reshape(B, T, H * dv).astype(dtype) * g


def chunked_gated_linear_attention(q, k, v, log_g):
    B, H, T, dk = q.shape
    dv = v.shape[-1]
    n = T // LA_CHUNK

    def to_chunks(a):
        return jnp.moveaxis(a.astype(jnp.float32).reshape(B, H, n, LA_CHUNK, a.shape[-1]), 2, 0)

    causal = jnp.tril(jnp.ones((LA_CHUNK, LA_CHUNK), dtype=bool))[:, :, None]

    def step(state, inp):
        qc, kc, vc, gc = inp
        b = jnp.cumsum(gc, axis=2)
        o_inter = jnp.einsum('bhtd,bhdv->bhtv', qc * jnp.exp(b), state)
        decay = jnp.exp(jnp.where(causal, b[:, :, :, None, :] - b[:, :, None, :, :], -jnp.inf))
        scores = jnp.einsum('bhtd,bhtsd,bhsd->bhts', qc, decay, kc)
        o = o_inter + jnp.einsum('bhts,bhsv->bhtv', scores, vc)
        b_last = b[:, :, -1:, :]
        state = (jnp.exp(b_last[:, :, 0, :, None]) * state
                 + jnp.einsum('bhsd,bhsv->bhdv', kc * jnp.exp(b_last - b), vc))
        return state, o

    s0 = jnp.zeros((B, H, dk, dv), jnp.float32)
    _, o = lax.scan(step, s0, (to_chunks(q), to_chunks(k), to_chunks(v), to_chunks(log_g)))
    return jnp.moveaxis(o, 0, 2).reshape(B, H, T, dv)


def partial_rope(x, cos, sin):
    half = ROT_DIM // 2
    c = cos[None, :, None, :].astype(x.dtype)
    s = sin[None, :, None, :].astype(x.dtype)
    x1 = x[..., :half]
    x2 = x[..., half:ROT_DIM]
    return jnp.concatenate([x1 * c - x2 * s, x2 * c + x1 * s, x[..., ROT_DIM:]], axis=-1)


def masked_softmax(s, mask):
    s = jnp.where(mask, s.astype(jnp.float32), -1e30)
    return jnp.where(mask, jax.nn.softmax(s, axis=-1), 0.0)


def compress(k, pos, w1, w2):
    B, G, T, dh = k.shape
    c = k.reshape(B, G, T // CMP_STRIDE, CMP_STRIDE, dh)
    blocks = jnp.concatenate([c[:, :, :-1], c[:, :, 1:]], axis=3) + pos
    flat = blocks.reshape(B, G, blocks.shape[2], CMP_BLK * dh)
    return jax.nn.gelu(flat @ w1) @ w2


def nsa_attention(q, kc, vc, ks, vs, kw, vw, gates):
    B, T, H, dh = q.shape
    G, R = NSA_KV_GROUPS, NSA_REP
    ncmp = kc.shape[2]
    nslc = T // SLC_BLK
    n_sel = min(N_SEL, nslc)
    nq = T // Q_BLOCK
    cmp_start = jnp.arange(ncmp) * CMP_STRIDE
    cmp_end = cmp_start + CMP_BLK - 1
    blk = jnp.arange(nslc)
    overlap = ((cmp_start[:, None] < (blk[None, :] + 1) * SLC_BLK)
               & (cmp_end[:, None] >= blk[None, :] * SLC_BLK)).astype(jnp.float32)
    ks_b = ks.reshape(B, G, nslc, SLC_BLK, dh)
    vs_b = vs.reshape(B, G, nslc, SLC_BLK, dh)
    kw_pad = jnp.pad(kw, ((0, 0), (0, 0), (WIN, 0), (0, 0)))
    vw_pad = jnp.pad(vw, ((0, 0), (0, 0), (WIN, 0), (0, 0)))
    b_idx = jnp.arange(B)[:, None, None, None]
    g_idx = jnp.arange(G)[None, :, None, None]
    q_blocks = jnp.moveaxis(q.reshape(B, nq, Q_BLOCK, G, R, dh), 1, 0).transpose(0, 1, 3, 4, 2, 5)

    def block_fn(args):
        c, qb = args
        t = c * Q_BLOCK + jnp.arange(Q_BLOCK)
        mask_c = cmp_end[None, :] <= t[:, None]
        p_c = masked_softmax(jnp.einsum('bgrqd,bgnd->bgrqn', qb, kc), mask_c)
        o_c = jnp.einsum('bgrqn,bgnd->bgrqd', p_c.astype(vc.dtype), vc)
        imp = jnp.einsum('bgrqn,nj->bgqj', p_c, overlap)
        cur = t // SLC_BLK
        valid = blk[None, :] <= cur[:, None]
        forced = (blk[None, :] == 0) | (blk[None, :] == cur[:, None]) | (blk[None, :] == cur[:, None] - 1)
        score = jnp.where(valid & forced, 1e9, jnp.where(valid, imp, -1e9))
        top_s, top_i = lax.top_k(score, n_sel)
        k_sel = ks_b[b_idx, g_idx, top_i].reshape(B, G, Q_BLOCK, n_sel * SLC_BLK, dh)
        v_sel = vs_b[b_idx, g_idx, top_i].reshape(B, G, Q_BLOCK, n_sel * SLC_BLK, dh)
        tok = (top_i[..., None] * SLC_BLK + jnp.arange(SLC_BLK)).reshape(B, G, Q_BLOCK, n_sel * SLC_BLK)
        mask_s = (tok <= t[None, None, :, None]) & jnp.repeat(top_s > -1e8, SLC_BLK, axis=-1)
        p_s = masked_softmax(jnp.einsum('bgrqd,bgqmd->bgrqm', qb, k_sel), mask_s[:, :, None])
        o_s = jnp.einsum('bgrqm,bgqmd->bgrqd', p_s.astype(v_sel.dtype), v_sel)
        k_win = lax.dynamic_slice_in_dim(kw_pad, c * Q_BLOCK, WIN + Q_BLOCK, axis=2)
        v_win = lax.dynamic_slice_in_dim(vw_pad, c * Q_BLOCK, WIN + Q_BLOCK, axis=2)
        kp = c * Q_BLOCK - WIN + jnp.arange(WIN + Q_BLOCK)
        mask_w = (kp[None, :] <= t[:, None]) & (kp[None, :] > t[:, None] - WIN) & (kp[None, :] >= 0)
        p_w = masked_softmax(jnp.einsum('bgrqd,bgkd->bgrqk', qb, k_win), mask_w)
        o_w = jnp.einsum('bgrqk,bgkd->bgrqd', p_w.astype(v_win.dtype), v_win)
        return jnp.stack([o_c, o_s, o_w], axis=-1)

    out = lax.map(block_fn, (jnp.arange(nq), q_blocks))
    out = out.transpose(1, 0, 4, 2, 3, 5, 6).reshape(B, T, H, dh, 3)
    return jnp.einsum('bthdk,bthk->bthd', out, gates).reshape(B, T, H * dh)


def hybrid_mixer(h, lb, w_in, hg_norm_g, gla_w2, gla_b, gla_norm_g, cmp_pos_k, cmp_pos_v,
                 cmp_w1_k, cmp_w2_k, cmp_w1_v, cmp_w2_v, w_br_hg, w_br_gla, w_br_nsa, w_out, cos, sin):
    B, T, _ = h.shape
    (hq, hf, hi, hog, gq, gk, gv, glr, gog,
     nq_, nkc, nvc, nks, nvs, nkw, nvw, ngate, mgate) = split_cols(h @ w_in)
    lbh = lb.reshape(1, HG_HEADS, 1, HG_HEAD_DIM)
    log_f = jnp.logaddexp(jnp.log(lbh), jnp.log1p(-lbh)
                          + jax.nn.log_sigmoid(heads(hf, HG_HEADS).astype(jnp.float32)))
    o_hg = chunked_gated_linear_attention(jax.nn.silu(heads(hq, HG_HEADS)), -jnp.expm1(log_f),
                                          heads(hi, HG_HEADS), log_f)
    y_hg = head_rmsnorm(o_hg, hg_norm_g, h.dtype) * jax.nn.silu(hog)
    log_a = jax.nn.log_sigmoid((glr @ gla_w2 + gla_b).astype(jnp.float32)) / GLA_TAU
    o_gla = chunked_gated_linear_attention(heads(gq, GLA_HEADS) * GLA_DK ** -0.5, heads(gk, GLA_HEADS),
                                           heads(gv, GLA_HEADS), heads(log_a, GLA_HEADS))
    y_gla = head_rmsnorm(o_gla, gla_norm_g, h.dtype) * jax.nn.silu(gog)
    q = partial_rope(nq_.reshape(B, T, NSA_HEADS, NSA_DH), cos, sin) * NSA_DH ** -0.5

    def kv(z):
        return z.reshape(B, T, NSA_KV_GROUPS, NSA_DH)

    def to_g(z):
        return z.transpose(0, 2, 1, 3)

    kc = compress(to_g(partial_rope(kv(nkc), cos, sin)), cmp_pos_k, cmp_w1_k, cmp_w2_k)
    vc = compress(to_g(kv(nvc)), cmp_pos_v, cmp_w1_v, cmp_w2_v)
    ks = to_g(partial_rope(kv(nks), cos, sin))
    vs = to_g(kv(nvs))
    kw = to_g(partial_rope(kv(nkw), cos, sin))
    vw = to_g(kv(nvw))
    gates = jax.nn.sigmoid(ngate).reshape(B, T, NSA_HEADS, 3)
    y_nsa = nsa_attention(q, kc, vc, ks, vs, kw, vw, gates)
    g = jax.nn.sigmoid(mgate).reshape(B, T, N_BRANCH, D_MODEL)
    m = (g[:, :, 0] * (y_hg @ w_br_hg) + g[:, :, 1] * (y_gla @ w_br_gla)
         + g[:, :, 2] * (y_nsa @ w_br_nsa))
    return m @ w_out


def swiglu(h, w_gate, w_up, w_down):
    return (jax.nn.silu(h @ w_gate) * (h @ w_up)) @ w_down


def setup_inputs(seed: int = 0) -> dict:
    key = jax.random.key(seed)
    k = jax.random.split(key, 24)
    L, D = DEPTH, D_MODEL

    def nrm(kk, shape, scale):
        return jax.random.normal(kk, shape, jnp.float32) * scale

    def gain(kk, shape):
        return 1.0 + 0.02 * jax.random.normal(kk, shape, jnp.float32)

    return {
        'x': nrm(k[0], (BATCH, SEQ, D), 1.0),
        'norm1_g': gain(k[1], (L, D)),
        'w_in': nrm(k[2], (L, D, IN_COLS), D ** -0.5),
        'hg_lb': nrm(k[3], (L, HG_DIM), 0.5),
        'hg_norm_g': gain(k[4], (L, HG_DIM)),
        'gla_w2': nrm(k[5], (L, GLA_RANK, GLA_KDIM), GLA_RANK ** -0.5),
        'gla_b': nrm(k[6], (L, GLA_KDIM), 0.1),
        'gla_norm_g': gain(k[7], (L, GLA_VDIM)),
        'cmp_pos_k': nrm(k[8], (L, CMP_BLK, NSA_DH), 0.1),
        'cmp_pos_v': nrm(k[9], (L, CMP_BLK, NSA_DH), 0.1),
        'cmp_w1_k': nrm(k[10], (L, CMP_BLK * NSA_DH, CMP_HID), (CMP_BLK * NSA_DH) ** -0.5),
        'cmp_w2_k': nrm(k[11], (L, CMP_HID, NSA_DH), CMP_HID ** -0.5),
        'cmp_w1_v': nrm(k[12], (L, CMP_BLK * NSA_DH, CMP_HID), (CMP_BLK * NSA_DH) ** -0.5),
        'cmp_w2_v': nrm(k[13], (L, CMP_HID, NSA_DH), CMP_HID ** -0.5),
        'w_br_hg': nrm(k[14], (L, HG_DIM, D), HG_DIM ** -0.5),
        'w_br_gla': nrm(k[15], (L, GLA_VDIM, D), GLA_VDIM ** -0.5),
        'w_br_nsa': nrm(k[16], (L, NSA_DIM, D), NSA_DIM ** -0.5),
        'w_out': nrm(k[17], (L, D, D), D ** -0.5),
        'norm2_g': gain(k[18], (L, D)),
        'w_ffn_gate': nrm(k[19], (L, D, FF_DIM), D ** -0.5),
        'w_ffn_up': nrm(k[20], (L, D, FF_DIM), D ** -0.5),
        'w_ffn_down': nrm(k[21], (L, FF_DIM, D), FF_DIM ** -0.5),
        'final_norm_g': gain(k[22], (D,)),
    }


def reference(x, norm1_g, w_in, hg_lb, hg_norm_g, gla_w2, gla_b, gla_norm_g, cmp_pos_k, cmp_pos_v,
              cmp_w1_k, cmp_w2_k, cmp_w1_v, cmp_w2_v, w_br_hg, w_br_gla, w_br_nsa, w_out,
              norm2_g, w_ffn_gate, w_ffn_up, w_ffn_down, final_norm_g):
    T = x.shape[1]
    pos = jnp.arange(T, dtype=jnp.float32)
    inv_freq = ROPE_THETA ** (-jnp.arange(0, ROT_DIM, 2, dtype=jnp.float32) / ROT_DIM)
    ang = pos[:, None] * inv_freq[None, :]
    cos, sin = jnp.cos(ang), jnp.sin(ang)
    lbs = jnp.cumsum(jax.nn.softmax(hg_lb.astype(jnp.float32), axis=0), axis=0)
    lbs = lbs - lbs[0]
    for l in range(DEPTH):
        h = rmsnorm(x, norm1_g[l])
        x = x + hybrid_mixer(h, lbs[l], w_in[l], hg_norm_g[l], gla_w2[l], gla_b[l], gla_norm_g[l],
                             cmp_pos_k[l], cmp_pos_v[l], cmp_w1_k[l], cmp_w2_k[l], cmp_w1_v[l], cmp_w2_v[l],
                             w_br_hg[l], w_br_gla[l], w_br_nsa[l], w_out[l], cos, sin)
        h = rmsnorm(x, norm2_g[l])
        x = x + swiglu(h, w_ffn_gate[l], w_ffn_up[l], w_ffn_down[l])
    return rmsnorm(x, final_norm_g)
```

```python
from contextlib import ExitStack
import ml_dtypes
import numpy as np
import concourse.bass as bass
import concourse.mybir as mybir
from concourse.bass_utils import run_bass_kernel_spmd

F32 = mybir.dt.float32
BF16 = mybir.dt.bfloat16
AF = mybir.ActivationFunctionType
ALU = mybir.AluOpType
AX = mybir.AxisListType


class Prog:
    ENG = ('pe', 'dve', 'act', 'pool', 'sp')
    NDMA = 6

    def __init__(self, nc):
        self.nc = nc
        self.eng = {'pe': nc.tensor, 'dve': nc.vector, 'act': nc.scalar,
                    'pool': nc.gpsimd, 'sp': nc.sync}
        self.q = {k: [] for k in self.ENG}
        self.cnt = {}
        self.sem = {}
        for k in ('pe', 'dve', 'act', 'pool'):
            self.sem[k] = nc.alloc_semaphore(name='c_' + k)
            self.cnt[k] = 0
        self.dma_pool = {}
        for qn in ('sp', 'pool', 'act'):
            lst = []
            for i in range(self.NDMA):
                key = 'd_%s%d' % (qn, i)
                self.sem[key] = nc.alloc_semaphore(name=key)
                self.cnt[key] = 0
                lst.append(key)
            self.dma_pool[qn] = [lst, 0, {}]
        self.seen = {k: {} for k in self.ENG}
        self.last_w = {}
        self.readers = {}
        self.n_ops = 0
        self.n_waits = 0

    def _deps(self, e, r, w):
        toks = []
        for k in r:
            t = self.last_w.get(k)
            if t is not None:
                toks.append(t)
        for k in w:
            t = self.last_w.get(k)
            if t is not None:
                toks.append(t)
            toks.extend(self.readers.get(k, ()))
        best = {}
        for (s, v) in toks:
            if s == 'pe' and e == 'pe':
                continue
            if v > best.get(s, 0):
                best[s] = v
        out = []
        seen = self.seen[e]
        for s, v in best.items():
            if seen.get(s, 0) >= v:
                continue
            seen[s] = v
            out.append((s, v))
        return out

    def _commit(self, tok, r, w):
        for k in w:
            self.last_w[k] = tok
            self.readers[k] = []
        for k in r:
            if k in w:
                continue
            self.readers.setdefault(k, []).append(tok)
            if len(self.readers[k]) > 24:
                best = {}
                for (s, v) in self.readers[k]:
                    if v > best.get(s, 0):
                        best[s] = v
                self.readers[k] = list(best.items())

    def op(self, e, fn, r=(), w=()):
        waits = self._deps(e, r, w)
        self.cnt[e] += 1
        tok = (e, self.cnt[e])
        self.seen[e][e] = max(self.seen[e].get(e, 0), 0)
        self._commit(tok, r, w)
        sem = self.sem
        self.n_ops += 1
        self.n_waits += len(waits)

        wl = [(sem[ws], wv) for (ws, wv) in waits]

        def emit(eng, wl=wl, fn=fn, s=sem[e]):
            for (wh, wv) in wl:
                eng.wait_ge(wh, wv)
            fn(eng).then_inc(s, 1)
        self.q[e].append(emit)
        return tok

    def dma(self, qn, out, in_, r=(), w=(), **kw):
        pool = self.dma_pool[qn]
        key = pool[0][pool[1] % self.NDMA]
        pool[1] += 1
        extra = []
        prev = pool[2].get(key)
        waits = self._deps(qn, r, w)
        if prev is not None and self.seen[qn].get(key, 0) < prev[1]:
            self.seen[qn][key] = prev[1]
            waits.append(prev)
        self.cnt[key] += 16
        tok = (key, self.cnt[key])
        pool[2][key] = tok
        self._commit(tok, r, w)
        sem = self.sem
        self.n_ops += 1
        self.n_waits += len(waits)

        wl = [(sem[ws], wv) for (ws, wv) in waits]

        def emit(eng, wl=wl, s=sem[key]):
            for (wh, wv) in wl:
                eng.wait_ge(wh, wv)
            eng.dma_start(out=out, in_=in_, **kw).then_inc(s, 16)
        self.q[qn].append(emit)
        return tok

    def wait_all(self, e, toks):
        sem = self.sem

        wl = [(sem[ws], wv) for (ws, wv) in toks]

        def emit(eng, wl=wl):
            for (wh, wv) in wl:
                eng.wait_ge(wh, wv)
        self.q[e].append(emit)

    def finish(self):
        toks = []
        for qn, pool in self.dma_pool.items():
            toks.extend(pool[2].values())
        for k in ('pe', 'dve', 'act', 'pool'):
            if self.cnt[k]:
                toks.append((k, self.cnt[k]))
        self.wait_all('sp', toks)

    def emit_all(self):
        nc = self.nc
        q = self.q
        with nc.Block() as block:
            @block.sync
            def _(e):
                for f in q['sp']:
                    f(e)

            @block.tensor
            def _(e):
                for f in q['pe']:
                    f(e)

            @block.vector
            def _(e):
                for f in q['dve']:
                    f(e)

            @block.scalar
            def _(e):
                for f in q['act']:
                    f(e)

            @block.gpsimd
            def _(e):
                for f in q['pool']:
                    f(e)


def _barrier(P):
    toks = []
    for qn, pool in P.dma_pool.items():
        toks.extend(pool[2].values())
    for k in ('pe', 'dve', 'act', 'pool'):
        if P.cnt[k]:
            toks.append((k, P.cnt[k]))
    for e in P.ENG:
        need = []
        for (s, v) in toks:
            if P.seen[e].get(s, 0) < v:
                P.seen[e][s] = v
                need.append((s, v))
        P.wait_all(e, need)
    P.last_w = {}
    P.readers = {}
    P.n_bar = getattr(P, 'n_bar', 0) + 1
    for k in ('pe', 'dve', 'act', 'pool'):
        P.sem[k] = P.nc.alloc_semaphore(name='c_%s_%d' % (k, P.n_bar))
        P.cnt[k] = 0
        for e in P.ENG:
            P.seen[e].pop(k, None)


T = 4096
D = 1024
NT = 32
L = 2
EPS = 1e-6
FF = 2816
NCOL = 7976
BIG = 8192.0
BF = ml_dtypes.bfloat16

C_HQ, C_HF, C_HI, C_HOG = 0, 512, 1024, 1536
C_GQ, C_GK, C_GV, C_GLR, C_GOG = 2048, 2304, 2560, 3072, 3088
C_NQ, C_NKC, C_NVC, C_NKS, C_NVS, C_NKW, C_NVW, C_NG, C_MG = 3600, 4112, 4240, 4368, 4496, 4624, 4752, 4880, 4904


def make_consts():
    c = {}
    c['ident'] = np.eye(128, dtype=np.float32).astype(BF)
    c['identf'] = np.eye(128, dtype=np.float32)
    s = np.arange(64)[:, None]
    t = np.arange(64)[None, :]
    tri = (s <= t).astype(np.float32)
    triu = (s > t).astype(np.float32)
    c['tri4'] = np.stack([tri, triu, -tri / 16.0, -triu / 16.0], axis=1).astype(np.float32)
    pos = np.arange(T, dtype=np.float32)
    inv_freq = (500000.0 ** (-np.arange(0, 16, 2, dtype=np.float32) / 16.0)).astype(np.float32)
    ang = pos[:, None] * inv_freq[None, :]
    cos, sin = np.cos(ang).astype(np.float32), np.sin(ang).astype(np.float32)
    CT = np.ones((128, T), np.float32)
    ST = np.zeros((128, T), np.float32)
    for r in range(128):
        d = r % 64
        if d < 8:
            CT[r] = cos[:, d]
            ST[r] = -sin[:, d]
        elif d < 16:
            CT[r] = cos[:, d - 8]
            ST[r] = sin[:, d - 8]
    c['ropeC'] = CT
    c['ropeS'] = ST
    p = np.arange(128)[:, None]
    q = np.arange(512)[None, :]
    mw = np.zeros((128, 8, 512), np.float32)
    for idx in range(8):
        key = 128 * (idx - 4) + p
        ok = (key <= q) & (key > q - 512)
        mw[:, idx, :] = np.where(ok, 0.0, -BIG)
    c['maskW'] = mw.astype(BF)
    tt_ = np.arange(T)[None, :]
    mc = np.zeros((128, 2, T), np.float32)
    for nt in range(2):
        n = nt * 128 + p
        ok = (16 * n + 31 <= tt_) & (n < 255)
        mc[:, nt, :] = np.where(ok, 0.0, -BIG)
    c['maskC'] = mc.astype(BF)
    E = np.zeros((64, 32, 128), np.float32)
    for kt in range(32):
        for key in range(128):
            E[2 * kt + key // 64, kt, key] = -BIG
    c['Eexp'] = E.astype(BF)
    ov = np.zeros((128, 2, 64), np.float32)
    for nt in range(2):
        for pp in range(128):
            n = nt * 128 + pp
            if n >= 255:
                continue
            cs, ce = 16 * n, 16 * n + 31
            for j in range(64):
                if cs < (j + 1) * 64 and ce >= j * 64:
                    ov[pp, nt, j] = 1.0
    c['ovl'] = ov.astype(BF)
    Wm = np.zeros((128, 128), np.float32)
    Wa = np.zeros((128, 128), np.float32)
    for pp in range(128):
        for x in range(128):
            dd = x - 64 - (1 if pp >= 64 else 0)
            if dd > 0:
                Wa[pp, x] = -1e9
            elif dd == 0:
                Wa[pp, x] = 1.1e9
            elif dd == -1:
                Wa[pp, x] = 1.2e9
            else:
                Wm[pp, x] = 1.0
    c['Wm'] = Wm
    c['Wa'] = Wa
    return c


CONST_SPECS = [('ident', [128, 128], 'bf'), ('identf', [128, 128], 'f'), ('tri4', [64, 4, 64], 'f'), ('ropeC', [128, T], 'f'), ('ropeS', [128, T], 'f'),
               ('maskW', [128, 8, 512], 'bf'), ('maskC', [128, 2, T], 'bf'), ('Eexp', [64, 32, 128], 'bf'),
               ('ovl', [128, 2, 64], 'bf'), ('Wm', [128, 128], 'f'), ('Wa', [128, 128], 'f')]

WEIGHT_SPECS = [('norm1_g', [L, D]), ('w_in', [L, D, NCOL]), ('hg_lb', [L, 512]), ('hg_norm_g', [L, 512]),
                ('gla_w2', [L, 16, 256]), ('gla_b', [L, 256]), ('gla_norm_g', [L, 512]),
                ('cmp_pos_k', [L, 32, 64]), ('cmp_pos_v', [L, 32, 64]), ('cmp_w1_k', [L, 2048, 128]),
                ('cmp_w2_k', [L, 128, 64]), ('cmp_w1_v', [L, 2048, 128]), ('cmp_w2_v', [L, 128, 64]),
                ('w_br_hg', [L, 512, D]), ('w_br_gla', [L, 512, D]), ('w_br_nsa', [L, 512, D]), ('w_out', [L, D, D]),
                ('norm2_g', [L, D]), ('w_ffn_gate', [L, D, FF]), ('w_ffn_up', [L, D, FF]), ('w_ffn_down', [L, FF, D]),
                ('final_norm_g', [D]), ('w_swap', [L, D, 896])]


def build(n_layers=L, dbg=None, dbg_outs=(), dbg_ins=()):
    nc = bass.Bass("TRN2", target_bir_lowering=False)
    P = Prog(nc)
    I = {}
    I['x'] = nc.dram_tensor("x", [T, D], F32, kind="ExternalInput").ap()
    for nm, shp in WEIGHT_SPECS:
        I[nm] = nc.dram_tensor(nm, shp, F32, kind="ExternalInput").ap()
    for nm, shp, ty in CONST_SPECS:
        I[nm] = nc.dram_tensor(nm, shp, BF16 if ty == 'bf' else F32, kind="ExternalInput").ap()
    y_out = nc.dram_tensor("y", [T, D], F32, kind="ExternalOutput").ap()

    def scr(name, shape, dt):
        kind = "ExternalOutput" if name in dbg_outs else ("ExternalInput" if name in dbg_ins else "Internal")
        return nc.dram_tensor(name, shape, dt, kind=kind).ap()

    S = {}
    S['QhT'] = scr('QhT', [512, T], BF16)
    S['KhT'] = scr('KhT', [512, T], BF16)
    S['LFh'] = scr('LFh', [T, 512], F32)
    S['Kh'] = scr('Kh', [T, 512], BF16)
    S['Vh'] = scr('Vh', [T, 512], BF16)
    S['OGh'] = scr('OGh', [T, 512], BF16)
    S['QgT'] = scr('QgT', [256, T], BF16)
    S['KgT'] = scr('KgT', [256, T], BF16)
    S['Kg'] = scr('Kg', [T, 256], BF16)
    S['Vg'] = scr('Vg', [T, 512], BF16)
    S['glrT'] = scr('glrT', [16, T], BF16)
    S['OGg'] = scr('OGg', [T, 512], BF16)
    S['QnT'] = scr('QnT', [512, T], BF16)
    S['KcTin'] = scr('KcTin', [128, T], BF16)
    S['VcTin'] = scr('VcTin', [128, T], BF16)
    S['KsT'] = scr('KsT', [128, T], BF16)
    S['Vs'] = scr('Vs', [T, 130], BF16)
    S['KwT'] = scr('KwT', [128, T], BF16)
    S['Vw'] = scr('Vw', [T, 130], BF16)
    S['Gn'] = scr('Gn', [T, 24], F32)
    S['MgT'] = scr('MgT', [3072, T], BF16)
    S['Yh'] = scr('Yh', [T, 512], BF16)
    S['Yg'] = scr('Yg', [T, 512], BF16)
    S['Yn'] = scr('Yn', [T, 512], BF16)
    S['X1'] = scr('X1', [T, D], F32)
    S['X2'] = scr('X2', [T, D], F32)
    S['aT'] = scr('aT', [FF, T], BF16)
    S['NS'] = scr('NS', [2, 64, 512], BF16)

    uid = [0]

    def sb(es, name, shape, dt):
        uid[0] += 1
        return es.enter_context(nc.sbuf_tensor('s%d_%s' % (uid[0], name), shape, dt))

    def ps(es, name, shape, dt=F32):
        uid[0] += 1
        return es.enter_context(nc.psum_tensor('p%d_%s' % (uid[0], name), shape, dt))

    def act(out, in_, func, r, w, **kw):
        P.op('act', lambda e: e.activation(out=out, in_=in_, func=func, **kw), r, w)

    def tt(eng, out, a, b, op, r, w):
        P.op(eng, lambda e: e.tensor_tensor(out=out, in0=a, in1=b, op=op), r, w)

    def ts(eng, out, a, s1, s2, op0, op1, r, w):
        if op1 is None:
            P.op(eng, lambda e: e.tensor_scalar(out=out, in0=a, scalar1=s1, scalar2=None, op0=op0), r, w)
        else:
            P.op(eng, lambda e: e.tensor_scalar(out=out, in0=a, scalar1=s1, scalar2=s2, op0=op0, op1=op1), r, w)

    def stt(eng, out, in0, scalar, in1, op0, op1, r, w):
        P.op(eng, lambda e: e.scalar_tensor_tensor(out=out, in0=in0, scalar=scalar, in1=in1, op0=op0, op1=op1), r, w)

    def cp(eng, out, in_, r, w):
        if eng == 'act':
            P.op('act', lambda e: e.activation(out=out, in_=in_, func=AF.Copy), r, w)
        else:
            P.op(eng, lambda e: e.tensor_copy(out=out, in_=in_), r, w)

    def mm(out, lhsT, rhs, st, sp, r, w, sgc=False):
        P.op('pe', lambda e: e.matmul(out, lhsT=lhsT, rhs=rhs, start=st, stop=sp, skip_group_check=sgc), r, w)

    def memset(eng, ap, val, w):
        P.op(eng, lambda e: e.memset(ap, val), (), w)

    def rstd_from_ss(ssv, outv, n, rk, wk):
        act(outv, ssv, AF.Sqrt, r=rk, w=wk, scale=1.0 / n, bias=EPS)
        P.op('dve', lambda e: e.reciprocal(out=outv, in_=outv), wk, wk)

    with ExitStack() as g_es:
        ident = sb(g_es, 'ident', [128, 128], BF16)
        identf = sb(g_es, 'identf', [128, 128], F32)
        P.dma('sp', ident[:], I['ident'][:, :], w=['ident'])
        P.dma('sp', identf[:], I['identf'][:, :], w=['identf'])

        def tr(out, in_, r, w):
            P.op('pe', lambda e: e.transpose(out=out, in_=in_, identity=ident[:]), list(r) + ['ident'], w)

        def stage_norm_T(es, x_src, g_row, hT, tiles=range(NT)):
            xt = [sb(es, 'nA_xt%d' % i, [128, D], F32) for i in range(2)]
            junk = sb(es, 'nA_junk', [128, D], BF16)
            hb = [sb(es, 'nA_hb%d' % i, [128, D], BF16) for i in range(2)]
            gb = sb(es, 'nA_gb', [128, D], F32)
            ss = [sb(es, 'nA_ss%d' % i, [128, 2], F32) for i in range(2)]
            pT = [ps(es, 'nA_pT%d' % i, [128, 4, 128], BF16) for i in range(2)]
            P.dma('sp', gb[:], g_row.partition_broadcast(128), w=['gb'])
            for i in tiles:
                b = i % 2
                P.dma('sp', xt[b][:], x_src[i * 128:(i + 1) * 128, :], w=['xt%d' % b])
                act(junk[:], xt[b][:], AF.Square, r=['xt%d' % b], w=['junk', 'ss%d' % b], accum_out=ss[b][:, 0:1])
                rstd_from_ss(ss[b][:, 0:1], ss[b][:, 1:2], D, ['ss%d' % b], ['rs%d' % b])
                stt('dve', hb[b][:], xt[b][:], ss[b][:, 1:2], gb[:], ALU.mult, ALU.mult,
                    r=['xt%d' % b, 'rs%d' % b, 'gb'], w=['hb%d' % b])
                for half in range(2):
                    for c in range(4):
                        cc = half * 4 + c
                        tr(pT[half][:, c, :], hb[b][:, cc * 128:(cc + 1) * 128], r=['hb%d' % b], w=['pT%d' % half])
                    cp('act' if half == 0 else 'dve', hT[:, half * 4:(half + 1) * 4, i * 128:(i + 1) * 128], pT[half][:],
                       r=['pT%d' % half], w=[('hT', i, half)])

        def stage_proj(l, hT):
            with ExitStack() as es:
                wb = [sb(es, 'pB_wb%d' % i, [128, 8, 512], BF16) for i in range(2)]
                ws = [sb(es, 'pB_ws%d' % i, [128, 8, 128], BF16) for i in range(2)]
                ropeC = sb(es, 'pB_ropeC', [128, T], F32)
                ropeS = sb(es, 'pB_ropeS', [128, T], F32)
                st_bf = [sb(es, 'pB_stb%d' % i, [128, 512], BF16) for i in range(3)]
                st_f = [sb(es, 'pB_stf%d' % i, [128, 512], F32) for i in range(3)]
                st_v = [sb(es, 'pB_stv%d' % i, [128, 2, 65], BF16) for i in range(2)]
                t1 = [sb(es, 'pB_t1%d' % i, [128, 512], F32) for i in range(2)]
                t2 = [sb(es, 'pB_t2%d' % i, [128, 512], F32) for i in range(2)]
                lbb = sb(es, 'pB_lbb', [128, 2, 512], F32)
                lbr = sb(es, 'pB_lbr', [128, 512], F32)
                omlb = sb(es, 'pB_omlb', [128, 512], F32)
                lbT = sb(es, 'pB_lbT', [4, 2, 128], F32)
                omlbT = sb(es, 'pB_omlbT', [128, 4], F32)
                ghg = sb(es, 'pB_ghg', [128, 512], F32)
                ggl = sb(es, 'pB_ggl', [128, 512], F32)
                psA = [ps(es, 'pB_psA%d' % i, [128, 512]) for i in range(3)]
                psB = [ps(es, 'pB_psB%d' % i, [128, 512]) for i in range(2)]
                P.dma('sp', ropeC[:], I['ropeC'][:, :], w=['ropeC'])
                P.dma('sp', ropeS[:], I['ropeS'][:, :], w=['ropeS'])
                P.dma('sp', ghg[:], I['hg_norm_g'][l].partition_broadcast(128), w=['ghg'])
                P.dma('sp', ggl[:], I['gla_norm_g'][l].partition_broadcast(128), w=['ggl'])
                for i in range(2):
                    memset('dve', st_v[i][:], 1.0, w=['stv%d' % i])
                if l == 0:
                    memset('dve', lbr[:], 0.0, w=['lbr'])
                    memset('dve', omlb[:], 1.0, w=['omlb'])
                    memset('dve', omlbT[:], 1.0, w=['omlbT'])
                else:
                    P.dma('sp', lbb[:, 0, :], I['hg_lb'][0].partition_broadcast(128), w=['lbb0'])
                    P.dma('sp', lbb[:, 1, :], I['hg_lb'][1].partition_broadcast(128), w=['lbb1'])
                    tt('dve', lbr[:], lbb[:, 1, :], lbb[:, 0, :], ALU.subtract, r=['lbb0', 'lbb1'], w=['lbr'])
                    act(omlb[:], lbr[:], AF.Sigmoid, r=['lbr'], w=['omlb'], scale=-1.0)
                    act(lbr[:], lbr[:], AF.Sigmoid, r=['lbr'], w=['lbr'])
                    P.dma('sp', lbT[:], I['hg_lb'].rearrange("l (h d) -> h l d", d=128), w=['lbT'])
                    tt('dve', lbT[:, 0, :], lbT[:, 1, :], lbT[:, 0, :], ALU.subtract, r=['lbT'], w=['lbT'])
                    act(lbT[:, 0, :], lbT[:, 0, :], AF.Sigmoid, r=['lbT'], w=['lbT'], scale=-1.0)
                    mm(psB[0][:, 0:4], lbT[:, 0, :], identf[0:4, 0:4], True, True, r=['lbT', 'identf'], w=['psB0'])
                    cp('dve', omlbT[:], psB[0][:, 0:4], r=['psB0'], w=['omlbT'])

                cnt = {'w': 0, 'a': 0, 'b': 0, 's': 0, 'f': 0, 'v': 0, 't': 0}

                def load_w(c0, n):
                    b = cnt['w'] % 2
                    cnt['w'] += 1
                    P.dma('pool', wb[b][:, :, 0:n], I['w_in'][l][:, c0:c0 + n].rearrange("(c p) n -> p c n", p=128),
                          w=['wb%d' % b])
                    return b

                def load_ws(s0):
                    b = cnt['s'] % 2
                    cnt['s'] += 1
                    P.dma('pool', ws[b][:], I['w_swap'][l][:, s0:s0 + 128].rearrange("(c p) n -> p c n", p=128),
                          w=['ws%d' % b])
                    return b

                def nxt(k, n):
                    v = cnt[k] % n
                    cnt[k] += 1
                    return v

                def hkeys(qg):
                    return [('hT', i, h) for i in range(qg * 4, qg * 4 + 4) for h in range(2)]

                def fm_job(c0, n, dst, post, swap0=None):
                    b = load_w(c0, n)
                    for mt in range((n + 127) // 128):
                        M = min(128, n - mt * 128)
                        sbi = load_ws(swap0 + mt * 128) if swap0 is not None else None
                        for qg in range(8):
                            a = nxt('a', 3)
                            for kc in range(8):
                                mm(psA[a][0:M, :], wb[b][:, kc, mt * 128:mt * 128 + M], hT[:, kc, qg * 512:(qg + 1) * 512],
                                   kc == 0, kc == 7, r=['wb%d' % b] + hkeys(qg), w=['psA%d' % a])
                            si = nxt('f', 3)
                            if swap0 is not None:
                                bb = nxt('b', 2)
                                for kc in range(8):
                                    mm(psB[bb][:, :], ws[sbi][:, kc, :], hT[:, kc, qg * 512:(qg + 1) * 512],
                                       kc == 0, kc == 7, r=['ws%d' % sbi] + hkeys(qg), w=['psB%d' % bb])
                                ti = nxt('t', 2)
                                tt('dve', t1[ti][:], psA[a][:], ropeC[:, qg * 512:(qg + 1) * 512], ALU.mult,
                                   r=['psA%d' % a, 'ropeC'], w=['t1%d' % ti])
                                tt('dve', t2[ti][:], psB[bb][:], ropeS[:, qg * 512:(qg + 1) * 512], ALU.mult,
                                   r=['psB%d' % bb, 'ropeS'], w=['t2%d' % ti])
                                tt('pool', st_bf[si][:], t1[ti][:], t2[ti][:], ALU.add,
                                   r=['t1%d' % ti, 't2%d' % ti], w=['stb%d' % si])
                            else:
                                post(psA[a][0:M, :], M, mt, si, 'psA%d' % a)
                            P.dma('sp', dst[mt * 128:mt * 128 + M, qg * 512:(qg + 1) * 512], st_bf[si][0:M, :],
                                  r=['stb%d' % si])

                def tm_job(c0, n, post):
                    b = load_w(c0, n)
                    for i in range(NT):
                        a = nxt('a', 3)
                        for kc in range(8):
                            mm(psA[a][:, 0:n], hT[:, kc, i * 128:(i + 1) * 128], wb[b][:, kc, 0:n],
                               kc == 0, kc == 7, r=['wb%d' % b, ('hT', i, 0), ('hT', i, 1)], w=['psA%d' % a])
                        post(psA[a][:, 0:n], i, 'psA%d' % a)

                def post_act(func, **kw):
                    def f(pv, M, mt, si, pk):
                        act(st_bf[si][0:M, :], pv, func, r=[pk], w=['stb%d' % si], **kw)
                    return f

                def post_kT(pv, M, mt, si, pk):
                    fi = nxt('t', 2)
                    act(t1[fi][:], pv, AF.Sigmoid, r=[pk], w=['t1%d' % fi], scale=-1.0)
                    ts('dve', st_bf[si][:], t1[fi][:], omlbT[:, mt:mt + 1], None, ALU.mult, None,
                       r=['t1%d' % fi, 'omlbT'], w=['stb%d' % si])

                def tm_simple(dst, ncols, func=AF.Copy, gmul=None, gk=None):
                    def f(pv, i, pk):
                        si = nxt('f', 3)
                        if gmul is None:
                            act(st_bf[si][:, 0:ncols], pv, func, r=[pk], w=['stb%d' % si])
                        else:
                            fi = nxt('t', 2)
                            act(t1[fi][:, 0:ncols], pv, func, r=[pk], w=['t1%d' % fi])
                            tt('dve', st_bf[si][:, 0:ncols], t1[fi][:, 0:ncols], gmul[:, 0:ncols], ALU.mult,
                               r=['t1%d' % fi, gk], w=['stb%d' % si])
                        P.dma('sp', dst[i * 128:(i + 1) * 128, :], st_bf[si][:, 0:ncols], r=['stb%d' % si])
                    return f

                def post_hf_tm(pv, i, pk):
                    fi = nxt('t', 2)
                    si = nxt('f', 3)
                    act(t1[fi][:], pv, AF.Sigmoid, r=[pk], w=['t1%d' % fi])
                    if l > 0:
                        tt('dve', t1[fi][:], t1[fi][:], omlb[:], ALU.mult, r=['t1%d' % fi, 'omlb'], w=['t1%d' % fi])
                        tt('dve', t1[fi][:], t1[fi][:], lbr[:], ALU.add, r=['t1%d' % fi, 'lbr'], w=['t1%d' % fi])
                    act(st_f[si][:], t1[fi][:], AF.Ln, r=['t1%d' % fi], w=['stf%d' % si])
                    ts('dve', st_bf[si][:], t1[fi][:], -1.0, 1.0, ALU.mult, ALU.add, r=['t1%d' % fi], w=['stb%d' % si])
                    P.dma('sp', S['LFh'][i * 128:(i + 1) * 128, :], st_f[si][:], r=['stf%d' % si])
                    P.dma('sp', S['Kh'][i * 128:(i + 1) * 128, :], st_bf[si][:], r=['stb%d' % si])

                def post_v(dst):
                    def f(pv, i, pk):
                        vi = nxt('v', 2)
                        cp('act', st_v[vi][:, :, 0:64], pv.rearrange("p (g d) -> p g d", g=2), r=[pk], w=['stv%d' % vi])
                        P.dma('sp', dst[i * 128:(i + 1) * 128, :], st_v[vi][:].rearrange("p g d -> p (g d)"),
                              r=['stv%d' % vi])
                    return f

                def post_gn(pv, i, pk):
                    si = nxt('f', 3)
                    act(st_f[si][:, 0:24], pv, AF.Sigmoid, r=[pk], w=['stf%d' % si])
                    P.dma('sp', S['Gn'][i * 128:(i + 1) * 128, :], st_f[si][:, 0:24], r=['stf%d' % si])

                fm_job(C_HQ, 512, S['QhT'], post_act(AF.Silu))
                fm_job(C_HF, 512, S['KhT'], post_kT)
                tm_job(C_HF, 512, post_hf_tm)
                tm_job(C_HI, 512, tm_simple(S['Vh'], 512))
                tm_job(C_HOG, 512, tm_simple(S['OGh'], 512, AF.Silu, ghg, 'ghg'))
                fm_job(C_GQ, 256, S['QgT'], post_act(AF.Copy, scale=0.125))
                fm_job(C_GK, 256, S['KgT'], post_act(AF.Copy))
                tm_job(C_GK, 256, tm_simple(S['Kg'], 256))
                tm_job(C_GV, 512, tm_simple(S['Vg'], 512))
                fm_job(C_GLR, 16, S['glrT'], post_act(AF.Copy))
                tm_job(C_GOG, 512, tm_simple(S['OGg'], 512, AF.Silu, ggl, 'ggl'))
                fm_job(C_NQ, 512, S['QnT'], None, swap0=0)
                fm_job(C_NKC, 128, S['KcTin'], None, swap0=512)
                fm_job(C_NVC, 128, S['VcTin'], post_act(AF.Copy))
                fm_job(C_NKS, 128, S['KsT'], None, swap0=640)
                tm_job(C_NVS, 128, post_v(S['Vs']))
                fm_job(C_NKW, 128, S['KwT'], None, swap0=768)
                tm_job(C_NVW, 128, post_v(S['Vw']))
                tm_job(C_NG, 24, post_gn)
                for blk in range(6):
                    fm_job(C_MG + blk * 512, 512, S['MgT'][blk * 512:(blk + 1) * 512, :], post_act(AF.Sigmoid))
                _barrier(P)

        class LinAttn:
            SC = 4

            def __init__(self, es, tag, dk, l, QT, KT, Kd, Vd, OG, Yd, gla, tri4):
                self.tag, self.dk, self.l, self.gla = tag, dk, l, gla
                self.QT, self.KT, self.Kd, self.Vd, self.OG, self.Yd = QT, KT, Kd, Vd, OG, Yd
                self.tri4 = tri4
                dk4 = 4 * dk
                self.dk4 = dk4
                SC = self.SC
                n = lambda s: 'la_%s_%s' % (tag, s)
                self.n = n
                if not gla:
                    self.lf = [sb(es, n('lf%d' % i), [64, SC, dk4], F32) for i in range(2)]
                else:
                    self.glr = sb(es, n('glr'), [16, T], BF16)
                    self.w2 = sb(es, n('w2'), [16, 256], BF16)
                    self.bb = sb(es, n('bb'), [1, 256], BF16)
                    self.ones = sb(es, n('ones'), [1, 64], BF16)
                    self.lfc = [sb(es, n('lfc%d' % i), [64, dk4], F32) for i in range(2)]
                    self.eu = sb(es, n('eu'), [64, dk4], F32)
                self.k = [sb(es, n('k%d' % i), [64, SC, dk4], BF16) for i in range(2)]
                self.v = [sb(es, n('v%d' % i), [64, SC, 512], BF16) for i in range(2)]
                self.og = [sb(es, n('og%d' % i), [64, SC, 512], BF16) for i in range(2)]
                self.qT = [sb(es, n('qT%d' % i), [dk, 4, SC * 64], BF16) for i in range(2)]
                self.kT = [sb(es, n('kT%d' % i), [dk, 4, SC * 64], BF16) for i in range(2)]
                self.yb = [sb(es, n('yb%d' % i), [64, SC, 512], BF16) for i in range(2)]
                self.eBL = sb(es, n('eBL'), [64, dk4], F32)
                self.kl = sb(es, n('kl'), [64, dk4], BF16)
                self.ebT = sb(es, n('ebT'), [dk, 4, 64], F32)
                self.enbT = sb(es, n('enbT'), [dk, 4, 64], F32)
                self.qeT = sb(es, n('qeT'), [dk, 4, 64], BF16)
                self.keT = sb(es, n('keT'), [dk, 4, 64], BF16)
                self.scm = sb(es, n('scm'), [64, 4, 64], BF16)
                self.St = sb(es, n('S'), [dk, 4, 128], F32)
                self.Sb = sb(es, n('Sb'), [dk, 4, 128], BF16)
                self.y1 = sb(es, n('y1'), [64, 4, 128], F32)
                self.junk = sb(es, n('junk'), [64, 128], BF16)
                self.ss = sb(es, n('ss'), [64, 8], F32)
                self.pBL = ps(es, n('pBL'), [128, 512])
                self.pbs = ps(es, n('pbs'), [128, 512])
                self.po = ps(es, n('po'), [128, 512])
                self.pdS = ps(es, n('pdS'), [128, 512])
                K = lambda s: n(s)
                memset('dve', self.St[:], 0.0, w=[K('S')])
                memset('dve', self.Sb[:], 0.0, w=[K('Sb')])
                if gla:
                    P.dma('sp', self.glr[:], S['glrT'][:, :], w=[K('glr')])
                    P.dma('pool', self.w2[:], I['gla_w2'][l], w=[K('w2')])
                    P.dma('pool', self.bb[:], I['gla_b'][l:l + 1, :], w=[K('bb')])
                    memset('dve', self.ones[:], 1.0, w=[K('ones')])

            def load(self, sc):
                K = self.n
                b = sc % 2
                SC, dk, dk4 = self.SC, self.dk, self.dk4
                t0, t1 = sc * SC * 64, (sc + 1) * SC * 64
                if not self.gla:
                    P.dma('sp', self.lf[b][:], S['LFh'][t0:t1, :].rearrange("(c p) n -> p c n", p=64), w=[K('lf%d' % b)])
                P.dma('sp', self.k[b][:], self.Kd[t0:t1, :].rearrange("(c p) n -> p c n", p=64), w=[K('k%d' % b)])
                P.dma('sp', self.v[b][:], self.Vd[t0:t1, :].rearrange("(c p) n -> p c n", p=64), w=[K('v%d' % b)])
                P.dma('sp', self.og[b][:], self.OG[t0:t1, :].rearrange("(c p) n -> p c n", p=64), w=[K('og%d' % b)])
                P.dma('sp', self.qT[b][:], self.QT[:, t0:t1].rearrange("(h d) t -> d h t", d=dk), w=[K('qT%d' % b)])
                P.dma('sp', self.kT[b][:], self.KT[:, t0:t1].rearrange("(h d) t -> d h t", d=dk), w=[K('kT%d' % b)])

            def step(self, c):
                K = self.n
                SC, dk, dk4 = self.SC, self.dk, self.dk4
                sc, cc = c // SC, c % SC
                b = sc % 2
                if c == 0:
                    self.load(0)
                if cc == 0 and (sc + 1) * SC * 64 < T:
                    self.load(sc + 1)
                tri4 = self.tri4
                if not self.gla:
                    lfc = self.lf[b][:, cc, :]
                    lfk = K('lf%d' % b)
                    tri, triu = tri4[:, 0, :], tri4[:, 1, :]
                else:
                    cb = c % 2
                    lfk = K('lfc%d' % cb)
                    mm(self.pBL[0:64, 0:256], self.glr[:, c * 64:(c + 1) * 64], self.w2[:], True, False,
                       r=[K('glr'), K('w2')], w=[K('pBL')])
                    mm(self.pBL[0:64, 0:256], self.ones[:], self.bb[:], False, True, r=[K('ones'), K('bb')], w=[K('pBL')])
                    act(self.eu[:], self.pBL[0:64, 0:256], AF.Exp, r=[K('pBL')], w=[K('eu')], scale=-1.0)
                    act(self.lfc[cb][:], self.eu[:], AF.Ln, r=[K('eu')], w=[lfk], bias=1.0)
                    lfc = self.lfc[cb][:]
                    tri, triu = tri4[:, 2, :], tri4[:, 3, :]
                kc_ = self.k[b][:, cc, :]
                vc_ = self.v[b][:, cc, :]
                mm(self.pBL[0:64, 0:dk4], triu, lfc, True, True, r=['tri4', lfk], w=[K('pBL')])
                act(self.eBL[:], self.pBL[0:64, 0:dk4], AF.Exp, r=[K('pBL')], w=[K('eBL')])
                tt('dve', self.kl[:], kc_, self.eBL[:], ALU.mult, r=[K('k%d' % b), K('eBL')], w=[K('kl')])
                pbT = self.pbs[0:dk, 0:256].rearrange("p (h t) -> p h t", h=4)
                psc = self.pbs[0:64, 256:512].rearrange("p (h t) -> p h t", h=4)
                for h in range(4):
                    mm(pbT[:, h, :], lfc[:, h * dk:(h + 1) * dk], tri, True, True, r=['tri4', lfk], w=[K('pbT')])
                act(self.ebT[:], pbT, AF.Exp, r=[K('pbT')], w=[K('ebT')])
                act(self.enbT[:], pbT, AF.Exp, r=[K('pbT')], w=[K('enbT')], scale=-1.0)
                tt('dve', self.qeT[:], self.qT[b][:, :, cc * 64:(cc + 1) * 64], self.ebT[:], ALU.mult,
                   r=[K('qT%d' % b), K('ebT')], w=[K('qeT')])
                tt('dve', self.keT[:], self.kT[b][:, :, cc * 64:(cc + 1) * 64], self.enbT[:], ALU.mult,
                   r=[K('kT%d' % b), K('enbT')], w=[K('keT')])
                for h in range(4):
                    mm(psc[:, h, :], self.keT[:, h, :], self.qeT[:, h, :], True, True, r=[K('keT'), K('qeT')], w=[K('psc')])
                tt('dve', self.scm[:], psc, tri4[:, 0, :].unsqueeze(1).to_broadcast([64, 4, 64]), ALU.mult,
                   r=[K('psc'), 'tri4'], w=[K('scm')])
                po = self.po[0:64, :].rearrange("p (h v) -> p h v", h=4)
                pdS = self.pdS[0:dk, :].rearrange("p (h v) -> p h v", h=4)
                for h in range(4):
                    mm(po[:, h, :], self.qeT[:, h, :], self.Sb[:, h, :], True, False, r=[K('qeT'), K('Sb')], w=[K('po')])
                    mm(po[:, h, :], self.scm[:, h, :], vc_[:, h * 128:(h + 1) * 128], False, True,
                       r=[K('scm'), K('v%d' % b)], w=[K('po')])
                for h in range(4):
                    mm(pdS[:, h, :], self.kl[:, h * dk:(h + 1) * dk], vc_[:, h * 128:(h + 1) * 128], True, True,
                       r=[K('kl'), K('v%d' % b)], w=[K('pdS')])
                for h in range(4):
                    stt('dve', self.St[:, h, :], self.St[:, h, :], self.ebT[:, h, 63:64], pdS[:, h, :], ALU.mult, ALU.add,
                        r=[K('S'), K('ebT'), K('pdS')], w=[K('S')])
                cp('act', self.Sb[:], self.St[:], r=[K('S')], w=[K('Sb')])
                for h in range(4):
                    act(self.junk[:], po[:, h, :], AF.Square, r=[K('po')], w=[K('junk'), K('ss')],
                        accum_out=self.ss[:, h:h + 1])
                rstd_from_ss(self.ss[:, 0:4], self.ss[:, 4:8], 128.0, [K('ss')], [K('rs')])
                tt('dve', self.y1[:], po, self.ss[:, 4:8].unsqueeze(2).to_broadcast([64, 4, 128]), ALU.mult,
                   r=[K('po'), K('rs')], w=[K('y1')])
                tt('pool', self.yb[b][:, cc, :], self.y1[:].rearrange("p h v -> p (h v)"), self.og[b][:, cc, :], ALU.mult,
                   r=[K('y1'), K('og%d' % b)], w=[K('yb%d' % b)])
                if cc == SC - 1:
                    t0, t1 = sc * SC * 64, (sc + 1) * SC * 64
                    P.dma('sp', self.Yd[t0:t1, :].rearrange("(c p) n -> p c n", p=64), self.yb[b][:], r=[K('yb%d' % b)])

        def stage_linattn(l):
            with ExitStack() as es:
                tri4 = sb(es, 'la_tri4', [64, 4, 64], F32)
                P.dma('sp', tri4[:], I['tri4'][:, :, :], w=['tri4'])
                hg = LinAttn(es, 'hg', 128, l, S['QhT'], S['KhT'], S['Kh'], S['Vh'], S['OGh'], S['Yh'], False, tri4)
                gl = LinAttn(es, 'gl', 64, l, S['QgT'], S['KgT'], S['Kg'], S['Vg'], S['OGg'], S['Yg'], True, tri4)
                import os
                for c in range(int(os.environ.get('LIN_CHUNKS', T // 64))):
                    if os.environ.get('LIN_SKIP', '') != 'hg':
                        hg.step(c)
                    if os.environ.get('LIN_SKIP', '') != 'gl':
                        gl.step(c)
                _barrier(P)

        def stage_nsa(l):
            with ExitStack() as es:
                kcT2 = sb(es, 'ns_kcT2', [128, 2, 256], BF16)
                VcX = sb(es, 'ns_VcX', [128, 2, 2, 129], BF16)
                ovl = sb(es, 'ns_ovl', [128, 2, 64], BF16)
                P.dma('sp', ovl[:], I['ovl'][:, :, :], w=['ovl'])
                with ExitStack() as es2:
                    kin = sb(es2, 'nc_kin', [128, T], BF16)
                    kTa = sb(es2, 'nc_kTa', [128, T], BF16)
                    kTb = sb(es2, 'nc_kTb', [128, T], BF16)
                    posT = sb(es2, 'nc_posT', [128, 32], F32)
                    pos2 = sb(es2, 'nc_pos2', [32, 128], F32)
                    w1 = sb(es2, 'nc_w1', [128, 32, 128], BF16)
                    w2d = sb(es2, 'nc_w2d', [128, 128], BF16)
                    hx = sb(es2, 'nc_hx', [128, 256], F32)
                    hx2 = sb(es2, 'nc_hx2', [128, 256], F32)
                    hbf = sb(es2, 'nc_hbf', [128, 256], BF16)
                    ph = ps(es2, 'nc_ph', [128, 512])
                    pk = ps(es2, 'nc_pk', [128, 512])
                    memset('dve', VcX[:], 1.0, w=['VcX'])
                    for nt in range(2):
                        for g in range(2):
                            cp('dve', VcX[:, nt, g, 65:129], ovl[:, nt, :], r=['ovl', 'VcX'], w=['VcX'])
                    memset('dve', hbf[:], 0.0, w=['hbf'])
                    for which in range(2):
                        src = S['KcTin'] if which == 0 else S['VcTin']
                        posd = I['cmp_pos_k' if which == 0 else 'cmp_pos_v'][l]
                        w1d = I['cmp_w1_k' if which == 0 else 'cmp_w1_v'][l]
                        w2s = I['cmp_w2_k' if which == 0 else 'cmp_w2_v'][l]
                        P.dma('sp', kin[:], src[:, :], w=['kin'])
                        for cpy in range(2):
                            P.dma('sp', pos2[:, cpy * 64:(cpy + 1) * 64], posd, w=['pos2_%d' % cpy])
                        mm(ph[:, 0:32], pos2[:, :], identf[0:32, 0:32], True, True, r=['pos2_0', 'pos2_1', 'identf'], w=['ph'])
                        cp('dve', posT[:, :], ph[:, 0:32], r=['ph'], w=['posT0', 'posT1'])
                        for cpy in range(2):
                            P.dma('pool', w1[cpy * 64:(cpy + 1) * 64, :, :], w1d.rearrange("(j d) h -> d j h", d=64),
                                  w=['w1_%d' % cpy])
                            P.dma('pool', w2d[:, cpy * 64:(cpy + 1) * 64], w2s, w=['w2d%d' % cpy])
                        k3 = kin[:].rearrange("p (n s) -> p n s", s=16)
                        tt('dve', kTa[:].rearrange("p (n s) -> p n s", s=16), k3,
                           posT[:, 0:16].unsqueeze(1).to_broadcast([128, 256, 16]), ALU.add,
                           r=['kin', 'posT0', 'posT1'], w=['kTa'])
                        tt('dve', kTb[:].rearrange("p (n s) -> p n s", s=16), k3,
                           posT[:, 16:32].unsqueeze(1).to_broadcast([128, 256, 16]), ALU.add,
                           r=['kin', 'posT0', 'posT1'], w=['kTb'])
                        a3 = kTa[:].rearrange("p (n s) -> p n s", s=16)
                        b3 = kTb[:].rearrange("p (n s) -> p n s", s=16)
                        for g in range(2):
                            pr = slice(g * 64, (g + 1) * 64)
                            for j in range(32):
                                rhs = a3[pr, 0:255, j] if j < 16 else b3[pr, 1:256, j - 16]
                                mm(ph[:, 0:255], w1[pr, j, :], rhs, j == 0, j == 31,
                                   r=['w1_0', 'w1_1', 'kTa', 'kTb'], w=['ph'])
                            cp('act', hx[:, 0:255], ph[:, 0:255], r=['ph'], w=['hx'])
                            tt('dve', hx2[:, 0:255], hx[:, 0:255], hx[:, 0:255], ALU.mult, r=['hx'], w=['hx2'])
                            ts('dve', hx2[:, 0:255], hx2[:, 0:255], 0.044715, 1.0, ALU.mult, ALU.add, r=['hx2'], w=['hx2'])
                            tt('dve', hx2[:, 0:255], hx2[:, 0:255], hx[:, 0:255], ALU.mult, r=['hx2', 'hx'], w=['hx2'])
                            act(hx2[:, 0:255], hx2[:, 0:255], AF.Sigmoid, r=['hx2'], w=['hx2'], scale=1.5957691216057308)
                            tt('dve', hbf[:, 0:255], hx2[:, 0:255], hx[:, 0:255], ALU.mult, r=['hx2', 'hx', 'hbf'], w=['hbf'])
                            if which == 0:
                                mm(pk[:, 0:256], w2d[:, :], hbf[:, :], True, True, r=['w2d0', 'w2d1', 'hbf'], w=['pk'])
                                cp('dve', kcT2[:, g, :], pk[:, 0:256], r=['pk'], w=['kcT2'])
                            else:
                                for nt in range(2):
                                    mm(pk[:, nt * 64:(nt + 1) * 64], hbf[:, nt * 128:(nt + 1) * 128], w2d[:, 0:64], True, True,
                                       r=['w2d0', 'hbf'], w=['pk'])
                                    cp('dve', VcX[:, nt, g, 0:64], pk[:, nt * 64:(nt + 1) * 64], r=['pk', 'VcX'], w=['VcX'])
                    _barrier(P)
                import os
                if dbg == 'cmp' or os.environ.get('NSA_STOP') == 'cmp':
                    return
                Ks2 = sb(es, 'ns_Ks2', [128, 2, T], BF16)
                Kw2 = sb(es, 'ns_Kw2', [128, 2, T], BF16)
                Vs = sb(es, 'ns_Vs', [128, NT, 130], BF16)
                Vw = sb(es, 'ns_Vw', [128, NT, 130], BF16)
                maskW = sb(es, 'ns_maskW', [128, 8, 512], BF16)
                maskC = sb(es, 'ns_maskC', [128, 2, T], BF16)
                msel = sb(es, 'ns_msel', [128, 2, NT, 512], BF16)
                Wm = sb(es, 'ns_Wm', [128, 128], F32)
                Wa = sb(es, 'ns_Wa', [128, 128], F32)
                qT = [sb(es, 'ns_qT%d' % i, [128, 4, 512], BF16) for i in range(2)]
                gt = [sb(es, 'ns_gt%d' % i, [128, 4, 24], F32) for i in range(2)]
                pex = [sb(es, 'ns_pex%d' % i, [128, 512], BF16) for i in range(3)]
                yacc = sb(es, 'ns_yacc', [128, 4, 512], F32)
                ybf = sb(es, 'ns_ybf', [128, 4, 512], BF16)
                imp = sb(es, 'ns_imp', [128, 4, 2, 64], F32)
                tmpi = sb(es, 'ns_tmpi', [128, 4, 64], F32)
                tmpo = [sb(es, 'ns_tmpo%d' % i, [128, 4, 64], F32) for i in range(2)]
                rd = [sb(es, 'ns_rd%d' % i, [128, 8], F32) for i in range(2)]
                sc_ = sb(es, 'ns_sc', [128, 64], F32)
                sc2 = sb(es, 'ns_sc2', [128, 64], F32)
                m8 = sb(es, 'ns_m8', [128, 16], F32)
                selb = sb(es, 'ns_selb', [128, 64], BF16)
                selT = sb(es, 'ns_selT', [64, 2, 512], BF16)
                pS = [ps(es, 'ns_pS%d' % i, [128, 512]) for i in range(3)]
                pacc = [ps(es, 'ns_pacc%d' % i, [128, 512]) for i in range(2)]
                pimp = ps(es, 'ns_pimp', [128, 512])
                pT = ps(es, 'ns_pT', [128, 4, 128], BF16)
                for cpy in range(2):
                    for g in range(2):
                        P.dma('sp', Ks2[cpy * 64:(cpy + 1) * 64, g, :], S['KsT'][g * 64:(g + 1) * 64, :], w=[('Ks2', cpy, g)])
                        P.dma('sp', Kw2[cpy * 64:(cpy + 1) * 64, g, :], S['KwT'][g * 64:(g + 1) * 64, :], w=[('Kw2', cpy, g)])
                Ksk = [('Ks2', c_, g_) for c_ in range(2) for g_ in range(2)]
                Kwk = [('Kw2', c_, g_) for c_ in range(2) for g_ in range(2)]
                P.dma('sp', Vs[:], S['Vs'].rearrange("(k p) n -> p k n", p=128), w=['Vs'])
                P.dma('sp', Vw[:], S['Vw'].rearrange("(k p) n -> p k n", p=128), w=['Vw'])
                P.dma('sp', maskW[:], I['maskW'][:, :, :], w=['maskW'])
                P.dma('sp', maskC[:], I['maskC'][:, :, :], w=['maskC'])
                P.dma('sp', Wm[:], I['Wm'][:, :], w=['Wm'])
                P.dma('sp', Wa[:], I['Wa'][:, :], w=['Wa'])
                cn = {'s': 0, 'p': 0, 'a': 0, 'o': 0, 'r': 0}

                def nx(k, n):
                    v = cn[k] % n
                    cn[k] += 1
                    return v

                import os
                for qg in range(int(os.environ.get('NSA_QG0', 0)), int(os.environ.get('NSA_QG1', 8))):
                    qb = qg % 2
                    P.dma('sp', qT[qb][:], S['QnT'][:, qg * 512:(qg + 1) * 512].rearrange("(p r) t -> r p t", r=128),
                          w=['qT%d' % qb])
                    P.dma('sp', gt[qb][:], S['Gn'][qg * 512:(qg + 1) * 512, :].rearrange("(a p) n -> p a n", p=128),
                          w=['gt%d' % qb])
                    qk, gk_ = 'qT%d' % qb, 'gt%d' % qb

                    def evac(ai, h, br, first):
                        acc3 = pacc[ai][:, 0:260].rearrange("p (a c) -> p a c", a=4)
                        ri = nx('r', 2)
                        ts('dve', rd[ri][:, 0:4], acc3[:, :, 64], 1e-30, None, ALU.add, None, r=['pacc%d' % ai], w=['rd%d' % ri])
                        P.op('dve', lambda e: e.reciprocal(out=rd[ri][:, 0:4], in_=rd[ri][:, 0:4]), ['rd%d' % ri], ['rd%d' % ri])
                        tt('dve', rd[ri][:, 4:8], rd[ri][:, 0:4], gt[qb][:, :, h * 3 + br], ALU.mult,
                           r=['rd%d' % ri, gk_], w=['rd%d' % ri])
                        ysl = yacc[:, :, h * 64:(h + 1) * 64]
                        cb_ = rd[ri][:, 4:8].unsqueeze(2).to_broadcast([128, 4, 64])
                        if first:
                            tt('dve', ysl, acc3[:, :, 0:64], cb_, ALU.mult, r=['pacc%d' % ai, 'rd%d' % ri], w=[('y', h)])
                        else:
                            oi = nx('o', 2)
                            tt('dve', tmpo[oi][:], acc3[:, :, 0:64], cb_, ALU.mult, r=['pacc%d' % ai, 'rd%d' % ri],
                               w=['tmpo%d' % oi])
                            tt('pool', ysl, ysl, tmpo[oi][:], ALU.add, r=['tmpo%d' % oi, ('y', h)], w=[('y', h)])
                        return ri

                    nts = [0, 1] if qg >= 4 else [0]
                    for h in range(8):
                        g, pair, j = h // 4, h // 2, h % 2
                        pr = slice(j * 64, (j + 1) * 64)
                        ai = nx('a', 2)
                        acc3 = pacc[ai][:, 0:260].rearrange("p (a c) -> p a c", a=4)
                        pim3 = pimp[:, 0:256].rearrange("p (a c) -> p a c", a=4)
                        for nt in nts:
                            si = nx('s', 3)
                            mm(pS[si][:], kcT2[pr, g, nt * 128:(nt + 1) * 128], qT[qb][pr, pair, :], True, False,
                               r=['kcT2', qk], w=['pS%d' % si])
                            mm(pS[si][:], ident[:], maskC[:, nt, qg * 512:(qg + 1) * 512], False, True,
                               r=['ident', 'maskC'], w=['pS%d' % si])
                            pi = nx('p', 3)
                            act(pex[pi][:], pS[si][:], AF.Exp, r=['pS%d' % si], w=['pex%d' % pi], scale=0.125)
                            for a in range(4):
                                mm(acc3[:, a, :], pex[pi][:, a * 128:(a + 1) * 128], VcX[:, nt, g, 0:65],
                                   nt == nts[0] and a == 0, nt == nts[-1], r=['pex%d' % pi, 'VcX'], w=['pacc%d' % ai], sgc=True)
                                mm(pim3[:, a, :], pex[pi][:, a * 128:(a + 1) * 128], VcX[:, nt, g, 65:129],
                                   nt == nts[0] and a == 0, nt == nts[-1], r=['pex%d' % pi, 'VcX'], w=['pimp'], sgc=True)
                        ri = evac(ai, h, 0, True)
                        rb = rd[ri][:, 0:4].unsqueeze(2).to_broadcast([128, 4, 64])
                        if h % 4 == 0:
                            tt('dve', imp[:, :, g, :], pim3, rb, ALU.mult, r=['pimp', 'rd%d' % ri], w=[('imp', g)])
                        else:
                            tt('dve', tmpi[:], pim3, rb, ALU.mult, r=['pimp', 'rd%d' % ri], w=['tmpi'])
                            tt('pool', imp[:, :, g, :], imp[:, :, g, :], tmpi[:], ALU.add, r=['tmpi', ('imp', g)], w=[('imp', g)])
                    skip = os.environ.get('NSA_SKIP', '')
                    for g in range(2 if 't' not in skip else 0):
                        for a in range(4):
                            qi = qg * 4 + a
                            w0 = 64 - 2 * qi
                            tt('dve', sc_[:], imp[:, a, g, :], Wm[:, w0:w0 + 64], ALU.mult, r=[('imp', g), 'Wm'], w=['sc'])
                            tt('dve', sc_[:], sc_[:], Wa[:, w0:w0 + 64], ALU.add, r=['sc', 'Wa'], w=['sc'])
                            memset('dve', sc_[:, 0:1], 1e9, w=['sc'])
                            P.op('dve', lambda e: e.max(out=m8[:, 0:8], in_=sc_[:]), ['sc'], ['m8'])
                            P.op('dve', lambda e: e.match_replace(out=sc2[:], in_to_replace=m8[:, 0:8], in_values=sc_[:],
                                                                  imm_value=-2e9), ['sc', 'm8'], ['sc2'])
                            P.op('dve', lambda e: e.max(out=m8[:, 8:16], in_=sc2[:]), ['sc2'], ['m8'])
                            ts('dve', m8[:, 15:16], m8[:, 15:16], -1e8, None, ALU.max, None, r=['m8'], w=['m8'])
                            ts('dve', selb[:], sc_[:], m8[:, 15:16], -BIG, ALU.is_lt, ALU.mult, r=['sc', 'm8'], w=['selb'])
                            tr(pT[0:64, 0, :], selb[:], r=['selb'], w=['pT'])
                            cp('act', selT[:, g, a * 128:(a + 1) * 128], pT[0:64, 0, :], r=['pT'], w=[('selT', g)])
                    if 's' not in skip:
                        for g in range(2):
                            P.dma('sp', S['NS'][g], selT[:, g, :], r=[('selT', g)], w=[('NS', g)])
                            for kt in range(4 * qg + 4):
                                for hf in range(2):
                                    P.dma('sp', msel[hf * 64:(hf + 1) * 64, g, kt, :],
                                          S['NS'][g, 2 * kt + hf, :].partition_broadcast(64),
                                          r=[('NS', g)], w=[('msel', g, kt, hf)])
                    for h in range(8):
                        g, pair, j = h // 4, h // 2, h % 2
                        pr = slice(j * 64, (j + 1) * 64)
                        ai = nx('a', 2)
                        acc3 = pacc[ai][:, 0:260].rearrange("p (a c) -> p a c", a=4)
                        nkt = 4 * qg + 4
                        for kt in range(nkt if 's' not in skip else 0):
                            si = nx('s', 3)
                            rel = kt - 4 * qg
                            mm(pS[si][:], Ks2[pr, g, kt * 128:(kt + 1) * 128], qT[qb][pr, pair, :], True, False,
                               r=Ksk + [qk], w=['pS%d' % si])
                            mm(pS[si][:], ident[:], msel[:, g, kt, :], False, rel < 0,
                               r=['ident', ('msel', g, kt, 0), ('msel', g, kt, 1)], w=['pS%d' % si])
                            if rel >= 0:
                                mm(pS[si][:], ident[:], maskW[:, 4 + rel, :], False, True, r=['ident', 'maskW'], w=['pS%d' % si])
                            pi = nx('p', 3)
                            act(pex[pi][:], pS[si][:], AF.Exp, r=['pS%d' % si], w=['pex%d' % pi], scale=0.125)
                            for a in range(4):
                                last = 4 * qg + a
                                if kt <= last:
                                    mm(acc3[:, a, :], pex[pi][:, a * 128:(a + 1) * 128], Vs[:, kt, g * 65:(g + 1) * 65],
                                       kt == 0 and a == 0, kt == last, r=['pex%d' % pi, 'Vs'], w=['pacc%d' % ai], sgc=True)
                        if 's' not in skip:
                            evac(ai, h, 1, False)
                        ai = nx('a', 2)
                        acc3 = pacc[ai][:, 0:260].rearrange("p (a c) -> p a c", a=4)
                        wfirst = [True]
                        for rel in range(-4, 4):
                            kt = 4 * qg + rel
                            if kt < 0 or 'w' in skip:
                                continue
                            si = nx('s', 3)
                            mm(pS[si][:], Kw2[pr, g, kt * 128:(kt + 1) * 128], qT[qb][pr, pair, :], True, False,
                               r=Kwk + [qk], w=['pS%d' % si])
                            mm(pS[si][:], ident[:], maskW[:, 4 + rel, :], False, True, r=['ident', 'maskW'], w=['pS%d' % si])
                            pi = nx('p', 3)
                            act(pex[pi][:], pS[si][:], AF.Exp, r=['pS%d' % si], w=['pex%d' % pi], scale=0.125)
                            for a in range(4):
                                if a - 4 <= rel <= a:
                                    first_kt = max(0, 4 * qg + a - 4)
                                    mm(acc3[:, a, :], pex[pi][:, a * 128:(a + 1) * 128], Vw[:, kt, g * 65:(g + 1) * 65],
                                       wfirst[0], rel == a, r=['pex%d' % pi, 'Vw'], w=['pacc%d' % ai], sgc=True)
                                    wfirst[0] = False
                        if 'w' not in skip:
                            evac(ai, h, 2, False)
                    cp('act', ybf[:], yacc[:], r=[('y', h_) for h_ in range(8)], w=['ybf'])
                    P.dma('sp', S['Yn'][qg * 512:(qg + 1) * 512, :].rearrange("(a p) n -> p a n", p=128), ybf[:], r=['ybf'])
                _barrier(P)

        def stage_merge(l, x_src, h2T):
            with ExitStack() as es:
                Wbr = [sb(es, 'mg_Wbr%d' % i, [128, 4, D], BF16) for i in range(3)]
                wout = sb(es, 'mg_wout', [128, 8, D], BF16)
                yin = [sb(es, 'mg_yin%d' % i, [128, 4, 512], BF16) for i in range(3)]
                yT = [sb(es, 'mg_yT%d' % i, [128, 4, 512], BF16) for i in range(3)]
                gsb = [sb(es, 'mg_g%d' % i, [128, 3, 512], BF16) for i in range(2)]
                m1 = sb(es, 'mg_m1', [128, 512], F32)
                m2 = sb(es, 'mg_m2', [128, 512], F32)
                m3 = sb(es, 'mg_m3', [128, 512], F32)
                mT = sb(es, 'mg_mT', [128, 8, 512], BF16)
                xt = [sb(es, 'mg_xt%d' % i, [128, D], F32) for i in range(2)]
                junk = sb(es, 'mg_junk', [128, D], BF16)
                hb = [sb(es, 'mg_hb%d' % i, [128, D], BF16) for i in range(2)]
                gb = sb(es, 'mg_gb', [128, D], F32)
                ss = [sb(es, 'mg_ss%d' % i, [128, 2], F32) for i in range(2)]
                pT = [ps(es, 'mg_pT%d' % i, [128, 4, 128], BF16) for i in range(2)]
                pM = [ps(es, 'mg_pM%d' % i, [128, 512]) for i in range(3)]
                pO = [ps(es, 'mg_pO%d' % i, [128, 512]) for i in range(2)]
                for bi, nm in enumerate(['w_br_hg', 'w_br_gla', 'w_br_nsa']):
                    P.dma('pool', Wbr[bi][:], I[nm][l].rearrange("(c p) n -> p c n", p=128), w=['Wbr%d' % bi])
                for hh in range(2):
                    P.dma('pool', wout[:, hh * 4:(hh + 1) * 4, :],
                          I['w_out'][l][hh * 512:(hh + 1) * 512, :].rearrange("(c p) n -> p c n", p=128), w=['wout%d' % hh])
                P.dma('sp', gb[:], I['norm2_g'][l].partition_broadcast(128), w=['gb'])
                ysrc = [S['Yh'], S['Yg'], S['Yn']]
                gT4 = S['MgT'].rearrange("(b c p) t -> p b c t", b=3, p=128)
                gcnt = 0
                tcnt = 0
                for tg in range(8):
                    for bi in range(3):
                        P.dma('sp', yin[bi][:], ysrc[bi][tg * 512:(tg + 1) * 512, :].rearrange("(a p) n -> p a n", p=128),
                              w=['yin%d' % bi])
                        for a in range(4):
                            pb = tcnt % 2
                            tcnt += 1
                            for c in range(4):
                                tr(pT[pb][:, c, :], yin[bi][:, a, c * 128:(c + 1) * 128], r=['yin%d' % bi], w=['pT%d' % pb])
                            cp('act' if pb == 0 else 'dve', yT[bi][:, :, a * 128:(a + 1) * 128], pT[pb][:],
                               r=['pT%d' % pb], w=['yT%d' % bi])
                    for ct in range(8):
                        gi = gcnt % 2
                        gcnt += 1
                        P.dma('sp', gsb[gi][:], gT4[:, :, ct, tg * 512:(tg + 1) * 512], w=['g%d' % gi])
                        for bi in range(3):
                            for c in range(4):
                                mm(pM[bi][:], Wbr[bi][:, c, ct * 128:(ct + 1) * 128], yT[bi][:, c, :], c == 0, c == 3,
                                   r=['Wbr%d' % bi, 'yT%d' % bi], w=['pM%d' % bi])
                        tt('dve', m1[:], pM[0][:], gsb[gi][:, 0, :], ALU.mult, r=['pM0', 'g%d' % gi], w=['m1'])
                        tt('dve', m2[:], pM[1][:], gsb[gi][:, 1, :], ALU.mult, r=['pM1', 'g%d' % gi], w=['m2'])
                        tt('dve', m3[:], pM[2][:], gsb[gi][:, 2, :], ALU.mult, r=['pM2', 'g%d' % gi], w=['m3'])
                        tt('pool', m1[:], m1[:], m2[:], ALU.add, r=['m1', 'm2'], w=['m1'])
                        tt('pool', mT[:, ct, :], m1[:], m3[:], ALU.add, r=['m1', 'm3'], w=[('mT', ct)])
                    mTk = [('mT', c_) for c_ in range(8)]
                    for a in range(4):
                        i = tg * 4 + a
                        b = i % 2
                        P.dma('sp', xt[b][:], x_src[i * 128:(i + 1) * 128, :], w=['xt%d' % b])
                        for hf in range(2):
                            for ct in range(8):
                                mm(pO[hf][:], mT[:, ct, a * 128:(a + 1) * 128], wout[:, ct, hf * 512:(hf + 1) * 512],
                                   ct == 0, ct == 7, r=mTk + ['wout0', 'wout1'], w=['pO%d' % hf])
                            tt('dve', xt[b][:, hf * 512:(hf + 1) * 512], xt[b][:, hf * 512:(hf + 1) * 512], pO[hf][:], ALU.add,
                               r=['pO%d' % hf, 'xt%d' % b], w=['xt%d' % b])
                        P.dma('sp', S['X1'][i * 128:(i + 1) * 128, :], xt[b][:], r=['xt%d' % b])
                        act(junk[:], xt[b][:], AF.Square, r=['xt%d' % b], w=['junk', 'ss%d' % b], accum_out=ss[b][:, 0:1])
                        rstd_from_ss(ss[b][:, 0:1], ss[b][:, 1:2], D, ['ss%d' % b], ['rs%d' % b])
                        stt('dve', hb[b][:], xt[b][:], ss[b][:, 1:2], gb[:], ALU.mult, ALU.mult,
                            r=['xt%d' % b, 'rs%d' % b, 'gb'], w=['hb%d' % b])
                        for half in range(2):
                            pb = tcnt % 2
                            tcnt += 1
                            for c in range(4):
                                cc = half * 4 + c
                                tr(pT[pb][:, c, :], hb[b][:, cc * 128:(cc + 1) * 128], r=['hb%d' % b], w=['pT%d' % pb])
                            cp('act' if pb == 0 else 'dve', h2T[:, half * 4:(half + 1) * 4, i * 128:(i + 1) * 128], pT[pb][:],
                               r=['pT%d' % pb], w=[('hT', i, half)])
                _barrier(P)

        def stage_ffn1(l, h2T):
            with ExitStack() as es:
                wg = [sb(es, 'f1_wg%d' % i, [128, 8, 128], BF16) for i in range(2)]
                wu = [sb(es, 'f1_wu%d' % i, [128, 8, 128], BF16) for i in range(2)]
                sg = [sb(es, 'f1_sg%d' % i, [128, 512], F32) for i in range(2)]
                ab = [sb(es, 'f1_ab%d' % i, [128, 512], BF16) for i in range(3)]
                pG = [ps(es, 'f1_pG%d' % i, [128, 512]) for i in range(2)]
                pU = [ps(es, 'f1_pU%d' % i, [128, 512]) for i in range(2)]
                k = 0
                for f in range(FF // 128):
                    wbi = f % 2
                    P.dma('pool', wg[wbi][:], I['w_ffn_gate'][l][:, f * 128:(f + 1) * 128].rearrange("(c p) n -> p c n", p=128),
                          w=['wg%d' % wbi])
                    P.dma('pool', wu[wbi][:], I['w_ffn_up'][l][:, f * 128:(f + 1) * 128].rearrange("(c p) n -> p c n", p=128),
                          w=['wu%d' % wbi])
                    for tg in range(8):
                        pb = k % 2
                        a3 = k % 3
                        k += 1
                        hk = [('hT', i, h_) for i in range(tg * 4, tg * 4 + 4) for h_ in range(2)]
                        for kc in range(8):
                            mm(pG[pb][:], wg[wbi][:, kc, :], h2T[:, kc, tg * 512:(tg + 1) * 512], kc == 0, kc == 7,
                               r=['wg%d' % wbi] + hk, w=['pG%d' % pb])
                        for kc in range(8):
                            mm(pU[pb][:], wu[wbi][:, kc, :], h2T[:, kc, tg * 512:(tg + 1) * 512], kc == 0, kc == 7,
                               r=['wu%d' % wbi] + hk, w=['pU%d' % pb])
                        act(sg[pb][:], pG[pb][:], AF.Silu, r=['pG%d' % pb], w=['sg%d' % pb])
                        tt('dve', ab[a3][:], sg[pb][:], pU[pb][:], ALU.mult, r=['sg%d' % pb, 'pU%d' % pb], w=['ab%d' % a3])
                        P.dma('sp', S['aT'][f * 128:(f + 1) * 128, tg * 512:(tg + 1) * 512], ab[a3][:], r=['ab%d' % a3])
                _barrier(P)

        def stage_ffn2(l, last):
            with ExitStack() as es:
                wd = sb(es, 'f2_wd', [128, 22, D], BF16)
                at = [sb(es, 'f2_at%d' % i, [128, 22, 512], BF16) for i in range(2)]
                xt = [sb(es, 'f2_xt%d' % i, [128, D], F32) for i in range(2)]
                ot = [sb(es, 'f2_ot%d' % i, [128, D], F32) for i in range(2)]
                junk = sb(es, 'f2_junk', [128, D], BF16)
                gb = sb(es, 'f2_gb', [128, D], F32)
                ss = [sb(es, 'f2_ss%d' % i, [128, 2], F32) for i in range(2)]
                pO = [ps(es, 'f2_pO%d' % i, [128, 512]) for i in range(4)]
                for q4 in range(11):
                    P.dma('pool', wd[:, q4 * 2:(q4 + 1) * 2, :],
                          I['w_ffn_down'][l][q4 * 256:(q4 + 1) * 256, :].rearrange("(c p) n -> p c n", p=128), w=[('wd', q4)])
                wdk = [('wd', q4) for q4 in range(11)]
                if last:
                    P.dma('sp', gb[:], I['final_norm_g'].partition_broadcast(128), w=['gb'])
                aT3 = S['aT'].rearrange("(f p) t -> p f t", p=128)
                k = 0
                for tg in range(8):
                    ab_ = tg % 2
                    P.dma('sp', at[ab_][:], aT3[:, :, tg * 512:(tg + 1) * 512], w=['at%d' % ab_])
                    for a in range(4):
                        i = tg * 4 + a
                        b = i % 2
                        P.dma('sp', xt[b][:], S['X1'][i * 128:(i + 1) * 128, :], w=['xt%d' % b])
                        for hf in range(2):
                            pi = k % 4
                            k += 1
                            for f in range(22):
                                mm(pO[pi][:], at[ab_][:, f, a * 128:(a + 1) * 128], wd[:, f, hf * 512:(hf + 1) * 512],
                                   f == 0, f == 21, r=['at%d' % ab_] + wdk, w=['pO%d' % pi])
                            tt('dve', xt[b][:, hf * 512:(hf + 1) * 512], xt[b][:, hf * 512:(hf + 1) * 512], pO[pi][:], ALU.add,
                               r=['pO%d' % pi, 'xt%d' % b], w=['xt%d' % b])
                        if not last:
                            P.dma('sp', S['X2'][i * 128:(i + 1) * 128, :], xt[b][:], r=['xt%d' % b])
                        else:
                            act(junk[:], xt[b][:], AF.Square, r=['xt%d' % b], w=['junk', 'ss%d' % b], accum_out=ss[b][:, 0:1])
                            rstd_from_ss(ss[b][:, 0:1], ss[b][:, 1:2], D, ['ss%d' % b], ['rs%d' % b])
                            stt('dve', ot[b][:], xt[b][:], ss[b][:, 1:2], gb[:], ALU.mult, ALU.mult,
                                r=['xt%d' % b, 'rs%d' % b, 'gb'], w=['ot%d' % b])
                            P.dma('sp', y_out[i * 128:(i + 1) * 128, :], ot[b][:], r=['ot%d' % b])
                _barrier(P)

        x_src = I['x']
        for l in range(n_layers):
            if dbg == 'nsa_only':
                stage_nsa(l)
                break
            if dbg == 'lin_only':
                stage_linattn(l)
                break
            with ExitStack() as es:
                hT = sb(es, 'hT_%d' % l, [128, 8, T], BF16)
                with ExitStack() as es2:
                    stage_norm_T(es2, x_src, I['norm1_g'][l], hT)
                    _barrier(P)
                stage_proj(l, hT)
            if dbg == 'proj':
                break
            stage_linattn(l)
            if dbg == 'lin':
                break
            stage_nsa(l)
            if dbg in ('nsa', 'cmp'):
                break
            with ExitStack() as es:
                h2T = sb(es, 'h2T_%d' % l, [128, 8, T], BF16)
                stage_merge(l, x_src, h2T)
                if dbg == 'merge':
                    break
                stage_ffn1(l, h2T)
            stage_ffn2(l, l == n_layers - 1)
            x_src = S['X2']
        P.finish()
        P.emit_all()
    return nc, P


def _swap_cols(w_in):
    out = np.zeros((L, D, 896), np.float32)
    groups = [(C_NQ, 512, 0), (C_NKC, 128, 512), (C_NKS, 128, 640), (C_NKW, 128, 768)]
    for (c0, n, s0) in groups:
        for j in range(n):
            d = j % 64
            if d < 8:
                out[:, :, s0 + j] = w_in[:, :, c0 + j + 8]
            elif d < 16:
                out[:, :, s0 + j] = w_in[:, :, c0 + j - 8]
    return out


_CACHE = {}


def kernel(**inputs):
    if 'nc' not in _CACHE:
        _CACHE['nc'] = build()[0]
        _CACHE['consts'] = make_consts()
    nc = _CACHE['nc']
    consts = _CACHE['consts']
    x = np.ascontiguousarray(np.asarray(inputs['x'], dtype=np.float32))
    shared = {}
    for nm, shp in WEIGHT_SPECS:
        if nm == 'w_swap':
            shared[nm] = _swap_cols(np.asarray(inputs['w_in'], dtype=np.float32))
        else:
            shared[nm] = np.ascontiguousarray(np.asarray(inputs[nm], dtype=np.float32))
    shared.update(consts)
    in_maps = []
    for b in range(8):
        m = dict(shared)
        m['x'] = x[b]
        in_maps.append(m)
    res = run_bass_kernel_spmd(nc, in_maps, core_ids=list(range(8)))
    return np.stack([np.asarray(r['y'], dtype=np.float32) for r in res.results], axis=0)
```
